# Optimizing a Trainium2 kernel written in Bass

```python
import jax, jax.numpy as jnp
from jax import lax
import numpy as np

D_MODEL = 2048
BATCH = 4
SEQ = 2048
DEPTH = 2
DEC_BATCH = 8
DEC_SEQ = 8
PAST_LEN = 16384
PAGE_SIZE = 128

N_MIXERS = 2
N_A = (DEPTH + 1) // 2
N_B = DEPTH // 2
MH = 8
DQK = D_MODEL // 2 // MH
DV = D_MODEL // MH
QK_TOT = MH * DQK
V_TOT = MH * DV
PA = 2 * QK_TOT + 2 * V_TOT + 2 * MH
MLSTM_CHUNK = 128
SB_HEADS = 16
SB_HD = D_MODEL // SB_HEADS
SB_BLOCK = 128
SB_SCALE = SB_HD ** -0.5
SB_BIAS_INIT = -9.0
D_FF = -(-8 * D_MODEL // (3 * 256)) * 256
EPS = 1e-6
N_PAGES = PAST_LEN // PAGE_SIZE
N_POOL = (DEC_BATCH * N_PAGES * 5) // 4

kernel_name = 'hybrid_mlstm_stickbreak_decoder_step'


def rmsnorm(x, g):
    xf = x.astype(jnp.float32)
    return (xf * lax.rsqrt(jnp.mean(xf * xf, axis=-1, keepdims=True) + EPS)).astype(x.dtype) * g


def modulate(x, g, shift, scale):
    return rmsnorm(x, g) * (1 + scale[:, None, :]) + shift[:, None, :]


def ada_params(c, w, b):
    return jnp.split(jax.nn.silu(c) @ w + b, 6, axis=-1)


def swiglu(h, w_in, w_out):
    g, u = jnp.split(h @ w_in, 2, axis=-1)
    return (jax.nn.silu(g) * u) @ w_out


def _mlstm_chunk(carry, xs):
    C, n, m = carry
    q, k, v, li, lf = xs
    L = q.shape[1]
    b = jnp.cumsum(lf, axis=1)
    m_t = b + jnp.maximum(m[:, None, :], lax.cummax(li - b, axis=1))
    d_inter = jnp.exp(b + m[:, None, :] - m_t)
    causal = jnp.tril(jnp.ones((L, L), dtype=bool))[None, :, :, None]
    log_d = b[:, :, None, :] - b[:, None, :, :] + li[:, None, :, :] - m_t[:, :, None, :]
    dmat = jnp.exp(jnp.where(causal, log_d, -jnp.inf))
    s = jnp.einsum('bthd,bshd->btsh', q, k) * dmat
    num = jnp.einsum('btsh,bshv->bthv', s, v) + d_inter[..., None] * jnp.einsum('bthd,bhdv->bthv', q, C)
    qn = jnp.sum(s, axis=2) + d_inter * jnp.einsum('bthd,bhd->bth', q, n)
    h = num / jnp.maximum(jnp.abs(qn), jnp.exp(-m_t))[..., None]
    m_end = m_t[:, -1]
    b_end = b[:, -1]
    w = jnp.exp(b_end[:, None, :] - b + li - m_end[:, None, :])
    carry_decay = jnp.exp(b_end + m - m_end)
    C_new = carry_decay[..., None, None] * C + jnp.einsum('bsh,bshd,bshv->bhdv', w, k, v)
    n_new = carry_decay[..., None] * n + jnp.einsum('bsh,bshd->bhd', w, k)
    return (C_new, n_new, m_end), h


def mlstm_mixer(h, C0, n0, m0, w_in, b_gate, g_head, w_out, chunk):
    bsz, L, _ = h.shape
    proj = (h @ w_in).astype(jnp.float32)
    o1 = QK_TOT
    o2 = 2 * QK_TOT
    o3 = o2 + V_TOT
    o4 = o3 + V_TOT
    o5 = o4 + MH
    q, k, v, og, gi, gf = jnp.split(proj, [o1, o2, o3, o4, o5], axis=-1)
    q = q.reshape(bsz, L, MH, DQK) * (DQK ** -0.5)
    k = k.reshape(bsz, L, MH, DQK)
    v = v.reshape(bsz, L, MH, DV)
    bg = b_gate.astype(jnp.float32)
    li = gi + bg[0]
    lf = jax.nn.log_sigmoid(gf + bg[1])
    nc = L // chunk

    def blocks(a):
        return a.reshape(bsz, nc, chunk, *a.shape[2:]).swapaxes(0, 1)

    carry0 = (C0.astype(jnp.float32), n0.astype(jnp.float32), m0.astype(jnp.float32))
    (C, n, m), hs = lax.scan(_mlstm_chunk, carry0, (blocks(q), blocks(k), blocks(v), blocks(li), blocks(lf)))
    hs = hs.swapaxes(0, 1).reshape(bsz, L, MH, DV)
    hs = hs * lax.rsqrt(jnp.mean(hs * hs, axis=-1, keepdims=True) + EPS) * g_head.reshape(MH, DV)
    out = (hs.reshape(bsz, L, V_TOT) * jax.nn.sigmoid(og)).astype(h.dtype) @ w_out
    return out.astype(h.dtype), C, n, m


def sb_project(h, w_in):
    bsz, L, _ = h.shape
    proj = (h @ w_in).astype(jnp.float32).reshape(bsz, L, 3, SB_HEADS, SB_HD)
    return proj[:, :, 0], proj[:, :, 1], proj[:, :, 2]


def sb_attend(q, k, v, qpos, bias):
    z = jnp.einsum('bqhd,bkhd->bhqk', q, k.astype(jnp.float32)) * SB_SCALE
    z = z + bias.astype(jnp.float32)[None, :, None, None]
    kpos = jnp.arange(k.shape[1])
    mask = kpos[None, :] < qpos[:, None]
    log_beta = jax.nn.log_sigmoid(z)
    log_1mb = jnp.where(mask, jax.nn.log_sigmoid(-z), 0.0)
    suffix = lax.cumsum(log_1mb, axis=3, reverse=True) - log_1mb
    a = jnp.where(mask, jnp.exp(log_beta + suffix), 0.0)
    return jnp.einsum('bhqk,bkhd->bqhd', a, v.astype(jnp.float32))


def sb_prompt(q, k, v, bias):
    bsz, L = q.shape[0], q.shape[1]
    nblk = L // SB_BLOCK
    qb = q.reshape(bsz, nblk, SB_BLOCK, SB_HEADS, SB_HD).swapaxes(0, 1)
    pos = jnp.arange(L).reshape(nblk, SB_BLOCK)
    ob = lax.map(lambda a: sb_attend(a[0], k, v, a[1], bias), (qb, pos))
    return ob.swapaxes(0, 1).reshape(bsz, L, SB_HEADS * SB_HD)


def gather_pages(pool, page_table):
    rows = jnp.take(pool, page_table, axis=0)
    return rows.reshape(page_table.shape[0], page_table.shape[1] * PAGE_SIZE, SB_HEADS, SB_HD)


def setup_inputs(seed: int = 0) -> dict:
    key = jax.random.key(seed)
    ks = jax.random.split(key, 24)
    f32 = jnp.float32

    def nrm(k, shape, s):
        return jax.random.normal(k, shape, f32) * s

    pool_perm = jax.random.permutation(ks[7], N_POOL)
    page_table = pool_perm[:DEC_BATCH * N_PAGES].reshape(DEC_BATCH, N_PAGES).astype(jnp.int32)
    b_gate_a = jnp.stack([nrm(ks[13], (N_A, MH), 0.1), 3.0 + nrm(ks[14], (N_A, MH), 0.5)], axis=1)
    return {
        'x_prompt': nrm(ks[0], (BATCH, SEQ, D_MODEL), 1.0),
        'x_sample': nrm(ks[1], (DEC_BATCH, DEC_SEQ, D_MODEL), 1.0),
        'state_C': nrm(ks[2], (N_A, DEC_BATCH, MH, DQK, DV), 0.1),
        'state_n': nrm(ks[3], (N_A, DEC_BATCH, MH, DQK), 0.1),
        'state_m': nrm(ks[4], (N_A, DEC_BATCH, MH), 0.5),
        'cache_k': nrm(ks[5], (N_B, N_POOL, PAGE_SIZE, SB_HEADS, SB_HD), 1.0),
        'cache_v': nrm(ks[6], (N_B, N_POOL, PAGE_SIZE, SB_HEADS, SB_HD), 1.0),
        'page_table': page_table,
        'c_prompt': nrm(ks[8], (BATCH, D_MODEL), 1.0),
        'c_sample': nrm(ks[9], (DEC_BATCH, D_MODEL), 1.0),
        'w_ada': nrm(ks[10], (DEPTH, D_MODEL, 6 * D_MODEL), 0.5 * D_MODEL ** -0.5),
        'b_ada': nrm(ks[11], (DEPTH, 6 * D_MODEL), 0.02),
        'g_norm': 1.0 + nrm(ks[12], (DEPTH, 4, D_MODEL), 0.02),
        'w_in_a': nrm(ks[15], (N_A, D_MODEL, PA), D_MODEL ** -0.5),
        'b_gate_a': b_gate_a,
        'g_head_a': 1.0 + nrm(ks[16], (N_A, V_TOT), 0.02),
        'w_out_a': nrm(ks[17], (N_A, V_TOT, D_MODEL), V_TOT ** -0.5),
        'w_in_b': nrm(ks[18], (N_B, D_MODEL, 3 * SB_HEADS * SB_HD), D_MODEL ** -0.5),
        'b_sb': SB_BIAS_INIT + nrm(ks[22], (N_B, SB_HEADS), 0.3),
        'w_out_b': nrm(ks[19], (N_B, SB_HEADS * SB_HD, D_MODEL), (SB_HEADS * SB_HD) ** -0.5),
        'w_ffn_in': nrm(ks[20], (DEPTH, D_MODEL, 2 * D_FF), D_MODEL ** -0.5),
        'w_ffn_out': nrm(ks[21], (DEPTH, D_FF, D_MODEL), D_FF ** -0.5),
    }


def reference(x_prompt, x_sample, state_C, state_n, state_m, cache_k, cache_v, page_table, c_prompt, c_sample, w_ada, b_ada, g_norm, w_in_a, b_gate_a, g_head_a, w_out_a, w_in_b, b_sb, w_out_b, w_ffn_in, w_ffn_out):
    xp, xs = x_prompt, x_sample
    bp, bs = xp.shape[0], xs.shape[0]
    Cp_l, np_l, mp_l, Cs_l, ns_l, ms_l = [], [], [], [], [], []
    kp_l, vp_l, ks_l, vs_l = [], [], [], []
    pos_s = PAST_LEN + jnp.arange(DEC_SEQ)
    for i in range(DEPTH):
        j = i // N_MIXERS
        mp_ = ada_params(c_prompt, w_ada[i], b_ada[i])
        ms_ = ada_params(c_sample, w_ada[i], b_ada[i])
        hp = modulate(xp, g_norm[i, 0], mp_[0], mp_[1])
        hs = modulate(xs, g_norm[i, 0], ms_[0], ms_[1])
        if i % N_MIXERS == 0:
            zC = jnp.zeros((bp, MH, DQK, DV), jnp.float32)
            zn = jnp.zeros((bp, MH, DQK), jnp.float32)
            zm = jnp.zeros((bp, MH), jnp.float32)
            op, Cp, n_p, m_p = mlstm_mixer(hp, zC, zn, zm, w_in_a[j], b_gate_a[j], g_head_a[j], w_out_a[j], min(MLSTM_CHUNK, hp.shape[1]))
            os_, Cs, n_s, m_s = mlstm_mixer(hs, state_C[j], state_n[j], state_m[j], w_in_a[j], b_gate_a[j], g_head_a[j], w_out_a[j], hs.shape[1])
            Cp_l.append(Cp); np_l.append(n_p); mp_l.append(m_p)
            Cs_l.append(Cs); ns_l.append(n_s); ms_l.append(m_s)
        else:
            qp, kp, vp = sb_project(hp, w_in_b[j])
            op = (sb_prompt(qp, kp, vp, b_sb[j]).astype(xp.dtype) @ w_out_b[j]).astype(xp.dtype)
            qs, kn, vn = sb_project(hs, w_in_b[j])
            k_all = jnp.concatenate([gather_pages(cache_k[j], page_table).astype(jnp.float32), kn], axis=1)
            v_all = jnp.concatenate([gather_pages(cache_v[j], page_table).astype(jnp.float32), vn], axis=1)
            os_ = sb_attend(qs, k_all, v_all, pos_s, b_sb[j]).reshape(bs, xs.shape[1], SB_HEADS * SB_HD)
            os_ = (os_.astype(xs.dtype) @ w_out_b[j]).astype(xs.dtype)
            kp_l.append(kp.astype(xp.dtype)); vp_l.append(vp.astype(xp.dtype))
            ks_l.append(kn.astype(xs.dtype)); vs_l.append(vn.astype(xs.dtype))
        xp = xp + mp_[2][:, None, :] * rmsnorm(op, g_norm[i, 1])
        xs = xs + ms_[2][:, None, :] * rmsnorm(os_, g_norm[i, 1])
        hp = modulate(xp, g_norm[i, 2], mp_[3], mp_[4])
        hs = modulate(xs, g_norm[i, 2], ms_[3], ms_[4])
        xp = xp + mp_[5][:, None, :] * rmsnorm(swiglu(hp, w_ffn_in[i], w_ffn_out[i]), g_norm[i, 3])
        xs = xs + ms_[5][:, None, :] * rmsnorm(swiglu(hs, w_ffn_in[i], w_ffn_out[i]), g_norm[i, 3])
    k_prompt = jnp.stack(kp_l)
    v_prompt = jnp.stack(vp_l)
    k_sample = jnp.stack(ks_l)
    v_sample = jnp.stack(vs_l)
    C_prompt = jnp.stack(Cp_l)
    n_prompt = jnp.stack(np_l)
    m_prompt = jnp.stack(mp_l)
    C_sample = jnp.stack(Cs_l)
    n_sample = jnp.stack(ns_l)
    m_sample = jnp.stack(ms_l)
    return (xp, xs, k_prompt, v_prompt, k_sample, v_sample, C_prompt, n_prompt, m_prompt, C_sample, n_sample, m_sample)
```

```python
import os
import numpy as np
import ml_dtypes
import concourse.bass as bass
import concourse.mybir as mybir
from concourse.bass_utils import run_bass_kernel_spmd

F32 = mybir.dt.float32
BF16 = mybir.dt.bfloat16
I32 = mybir.dt.int32
AF = mybir.ActivationFunctionType
ALU = mybir.AluOpType
AX = mybir.AxisListType

FULL_CFG = dict(D=2048, SEQ=2048, TS=8, MH=8, DFF=5632, NPAGES=128, NPOOL=1280, DEPTH=2)
EPS = 1e-6


class Buf:
    __slots__ = ("name", "w", "r", "dram", "ds")

    def __init__(self, name, dram=False):
        self.name = name
        self.w = {}
        self.r = {}
        self.dram = dram
        self.ds = None


class DmaSem:
    def __init__(self, sem):
        self.sem = sem
        self.count = 0


class EngSem:
    def __init__(self, eng, sem):
        self.eng = eng
        self.sem = sem
        self.count = 0


SEM_LIMIT = int(os.environ.get("KSEMLIM", "12000"))


class Eng:
    def __init__(self, name, handle, sem, selfsync=True):
        self.name = name
        self.h = handle
        self.cur = EngSem(self, sem)
        self.epochs = [self.cur]
        self.seen = {}
        self.selfsync = selfsync

    @property
    def count(self):
        return sum(e.count for e in self.epochs)


import os
STOP = os.environ.get("KSTOP", "")


def ckpt(P, name):
    if STOP == name:
        P.skip = True


class Prog:
    def __init__(self, nc, sems):
        self.nc = nc
        self.free_sems = list(sems)
        self.engs = {}
        self.dsems = []
        self.skip = False

    def add_engine(self, name, handle, selfsync=True):
        e = Eng(name, handle, self.free_sems.pop(), selfsync)
        self.engs[name] = e
        return e

    def _deps(self, reads, writes):
        need = []
        for b in reads:
            need.extend(b.w.items())
            if b.name.startswith("ps_"):
                need.extend(b.r.items())
        for b in writes:
            need.extend(b.w.items())
            need.extend(b.r.items())
        return need

    def _emit_waits(self, eng, need):
        best = {}
        for (k, v) in need:
            if isinstance(k, DmaSem):
                v = k.count
            if v > best.get(k, 0):
                best[k] = v
        for k, v in best.items():
            if isinstance(k, EngSem) and k.eng is eng and not eng.selfsync:
                continue
            if eng.seen.get(k, 0) >= v:
                continue
            eng.h.wait_ge(k.sem, v)
            eng.seen[k] = v

    def op(self, eng, fn, reads=(), writes=()):
        if self.skip:
            return None
        eng = self.engs[eng] if isinstance(eng, str) else eng
        self._emit_waits(eng, self._deps(reads, writes))
        if eng.cur.count >= SEM_LIMIT:
            eng.cur = EngSem(eng, self.free_sems.pop())
            eng.epochs.append(eng.cur)
        inst = fn(eng.h)
        ep = eng.cur
        ep.count += 1
        inst.then_inc(ep.sem, 1)
        for b in reads:
            b.r[ep] = ep.count
        for b in writes:
            b.w = {ep: ep.count}
            b.r = {}
        return inst

    def dma(self, eng, out_ap, in_ap, reads=(), writes=(), group=None, **kw):
        if self.skip:
            return None
        eng = self.engs[eng] if isinstance(eng, str) else eng
        sbs = [b for b in list(reads) + list(writes) if not b.dram]
        assert len(sbs) == 1, [b.name for b in list(reads) + list(writes)]
        sbb = sbs[0]
        if sbb.ds is None:
            sbb.ds = DmaSem(self.free_sems.pop())
            self.dsems.append(sbb.ds)
        ds = sbb.ds
        self._emit_waits(eng, self._deps(reads, writes))
        inst = eng.h.dma_start(out=out_ap, in_=in_ap, **kw)
        ds.count += 16
        inst.then_inc(ds.sem, 16)
        for b in reads:
            b.r[ds] = ds.count
        for b in writes:
            if b.dram:
                b.w = {k: v for k, v in b.w.items() if isinstance(k, DmaSem)}
                b.w[ds] = ds.count
            else:
                b.w = {ds: ds.count}
            b.r = {}
        return inst

    def idma(self, out_ap, in_ap, idx_ap, reads=(), writes=()):
        if self.skip:
            return None
        eng = self.engs["pool"]
        sbs = [b for b in list(writes) if not b.dram]
        sbb = sbs[0]
        if sbb.ds is None:
            sbb.ds = DmaSem(self.free_sems.pop())
            self.dsems.append(sbb.ds)
        ds = sbb.ds
        self._emit_waits(eng, self._deps(reads, writes))
        inst = eng.h.indirect_dma_start(out=out_ap, out_offset=None, in_=in_ap,
                                        in_offset=bass.IndirectOffsetOnAxis(ap=idx_ap, axis=0))
        ds.count += 16
        inst.then_inc(ds.sem, 16)
        for b in reads:
            b.r[ds] = ds.count
        for b in writes:
            b.w = {ds: ds.count}
            b.r = {}
        return inst

    def collective(self, fn, reads=(), writes=()):
        if self.skip:
            return None
        eng = self.engs["pool"]
        ds = DmaSem(self.free_sems.pop())
        self.dsems.append(ds)
        self._emit_waits(eng, self._deps(reads, writes))
        inst = fn(eng.h)
        ds.count += 1
        inst.then_inc(ds.sem, 1)
        for b in reads:
            b.r[ds] = ds.count
        for b in writes:
            b.w = {ds: ds.count}
            b.r = {}
        return inst

    def wait_all_dma(self, eng):
        eng = self.engs[eng]
        for ds in self.dsems:
            if ds.count > 0:
                eng.h.wait_ge(ds.sem, ds.count)

    def barrier(self):
        if self.skip:
            return
        for e in self.engs.values():
            for oe in self.engs.values():
                if oe is e:
                    continue
                for o in oe.epochs:
                    if o.count > 0 and e.seen.get(o, 0) < o.count:
                        e.h.wait_ge(o.sem, o.count)
                        e.seen[o] = o.count
            for ds in self.dsems:
                if ds.count > 0 and e.seen.get(ds, 0) < ds.count:
                    e.h.wait_ge(ds.sem, ds.count)
                    e.seen[ds] = ds.count


class Rot:
    def __init__(self, items):
        self.items = items
        self.i = 0

    def next(self):
        it = self.items[self.i % len(self.items)]
        self.i += 1
        return it


def build_program(cfg):
    D = cfg["D"]; SEQ = cfg["SEQ"]; TS = cfg["TS"]; MH = cfg["MH"]; DFF = cfg["DFF"]
    NPAGES = cfg["NPAGES"]; NPOOL = cfg["NPOOL"]
    KC = D // 128
    TO = SEQ // 2
    TC = TO
    T = TO + TS
    NT = TO // 128
    DQK = 128; DV = 256
    QK = MH * DQK; VT = MH * DV
    assert VT == D and QK == D // 2
    PA = 2 * QK + 2 * VT + 2 * MH
    H = D // 128
    HD = 128
    FC = DFF // 128
    NTOT = TC + T
    SB_SCALE = HD ** -0.5
    QS = DQK ** -0.5

    nc = bass.Bass("TRN2", target_bir_lowering=False)

    def din(name, shape, dt=F32):
        return nc.dram_tensor(name, list(shape), dt, kind="ExternalInput").ap()

    def dout(name, shape, dt=F32):
        return nc.dram_tensor(name, list(shape), dt, kind="ExternalOutput").ap()

    x_own = din("x_own", [T, D]); x_pre = din("x_pre", [TC, D]); c2 = din("c2", [2, D])
    flags = din("flags", [MH, 2]); vflag = din("vflag", [128, 1])
    st_C = din("st_C", [MH, DQK, DV]); st_n = din("st_n", [MH, DQK]); st_m = din("st_m", [MH, 1])
    cache_k = din("cache_k", [NPOOL, 128, D]); cache_v = din("cache_v", [NPOOL, 128, D])
    ptab = din("ptab", [1, NPAGES], I32)
    w_ada = din("w_ada", [2, D, 6 * D]); b_ada = din("b_ada", [2, 6 * D]); g_norm = din("g_norm", [2, 4, D])
    w_in_a = din("w_in_a", [D, PA]); b_gate = din("b_gate", [MH, 2]); g_head = din("g_head", [VT])
    w_out_a = din("w_out_a", [VT, D]); w_in_b = din("w_in_b", [D, 3 * D]); b_sb = din("b_sb", [H])
    w_out_b = din("w_out_b", [D, D]); w_ffn_in = din("w_ffn_in", [2, D, 2 * DFF]); w_ffn_out = din("w_ffn_out", [2, DFF, D])
    c_ident = din("c_ident", [128, 128]); c_m1 = din("c_m1", [128, 128]); c_uincl = din("c_uincl", [128, 128])
    c_lstr = din("c_lstr", [128, 128]); c_am = din("c_am", [128, 4 * 512]); c_ams = din("c_ams", [8, H * 8])

    y_own = dout("y_own", [T, D]); k_own = dout("k_own", [TO, D]); v_own = dout("v_own", [TO, D])
    k_s = dout("k_s", [TS, D]); v_s = dout("v_s", [TS, D])
    C_p = dout("C_p", [MH, DQK, DV]); n_p = dout("n_p", [MH, DQK]); m_p = dout("m_p", [MH, 1])
    C_s = dout("C_s", [MH, DQK, DV]); n_s = dout("n_s", [MH, DQK]); m_s = dout("m_s", [MH, 1])

    x_d = nc.dram_tensor("x_d", [T, D], F32).ap()
    o_d = nc.dram_tensor("o_d", [T, D], F32).ap()
    ada_d = nc.dram_tensor("ada_d", [2, 2, 6 * D], F32).ap()
    K_d = nc.dram_tensor("K_d", [NTOT, QK], BF16).ap()
    V_d = nc.dram_tensor("V_d", [NTOT, VT], BF16).ap()
    G_d = nc.dram_tensor("G_d", [T, VT], BF16).ap()
    CH_BYTES = 2 * 1024 * 1024
    nkc = max(1, (D * TO * 2) // CH_BYTES)
    KROWS = D // nkc
    VROWS = TO // nkc
    assert KROWS % 128 == 0 and VROWS % 128 == 0
    bk_in = [nc.dram_tensor("bk_in%d" % i, [KROWS, TO], BF16) for i in range(nkc)]
    bk_out = [nc.dram_tensor("bk_out%d" % i, [2 * KROWS, TO], BF16) for i in range(nkc)]
    bv_in = [nc.dram_tensor("bv_in%d" % i, [VROWS, D], BF16) for i in range(nkc)]
    bv_out = [nc.dram_tensor("bv_out%d" % i, [2 * VROWS, D], BF16) for i in range(nkc)]

    import contextlib
    es = contextlib.ExitStack()
    with es:
        scopes = [es]

        def sb(name, shape, dt):
            return scopes[-1].enter_context(nc.sbuf_tensor(name, list(shape), dt))

        def push_scope():
            scopes.append(contextlib.ExitStack())

        def pop_scope():
            P.barrier()
            scopes.pop().close()

        def psum(name, shape, dt):
            return es.enter_context(nc.psum_tensor(name, list(shape), dt))

        sems = [es.enter_context(nc.semaphore("s%d" % i)) for i in range(100)]
        P = Prog(nc, sems)
        P.add_engine("pe", nc.tensor, selfsync=False)
        P.add_engine("act", nc.scalar)
        P.add_engine("dve", nc.vector)
        P.add_engine("pool", nc.gpsimd)
        P.add_engine("sp", nc.sync)

        WSLOT = max(KC * 512, FC * 128)
        wslots = Rot([(sb("wslot%d" % i, [128, WSLOT], BF16), Buf("wslot%d" % i)) for i in range(2)])
        buf1 = sb("buf1", [128, KC * T], BF16); B1 = Buf("buf1")
        B2N = max(KC * T, FC * (T - 512 if T > 512 else T))
        buf2 = sb("buf2", [128, B2N], BF16); B2 = Buf("buf2")
        hT = buf1[:, :].rearrange("p (k t) -> p k t", t=T)
        ident_f = sb("ident_f", [128, 128], F32); ident_b = sb("ident_b", [128, 128], BF16)
        m1_b = sb("m1_b", [128, 128], BF16); uincl_b = sb("uincl_b", [128, 128], BF16); lstr_b = sb("lstr_b", [128, 128], BF16)
        am_b = sb("am_b", [128, 4 * 512], BF16); ams_b = sb("ams_b", [8, H * 8], BF16)
        ones_f = sb("ones_f", [128, 128], F32); ones_b = sb("ones_b", [128, 128], BF16)
        CONST = Buf("const")
        adaT = sb("adaT", [128, 2 * 2 * 6 * KC], F32)
        gnT = sb("gnT", [128, 128], F32)
        modA = sb("modA", [128, 2 * 2 * 2 * KC], F32)
        ADA = Buf("ada")
        xs_pool = Rot([(sb("xs%d" % i, [128, D], F32), Buf("xs%d" % i)) for i in range(2)])
        xb_pool = Rot([(sb("xb%d" % i, [128, D], BF16), Buf("xb%d" % i)) for i in range(2)])
        st_pool = Rot([(sb("stg%d" % i, [128, 512], F32), Buf("stg%d" % i)) for i in range(2)])
        sbf_pool = Rot([(sb("sbf%d" % i, [128, 512], BF16), Buf("sbf%d" % i)) for i in range(3)])
        sm_pool = Rot([(sb("sm%d" % i, [128, 64], F32), Buf("sm%d" % i)) for i in range(6)])
        gb_t = sb("gb_t", [128, D], F32); GB = Buf("gb")
        pf_pool = Rot([(psum("pf%d" % i, [128, 512], F32), Buf("ps_pf%d" % i)) for i in range(4)])
        pb_pool = Rot([(psum("pb%d" % i, [128, 1024], BF16), Buf("ps_pb%d" % i)) for i in range(2)])
        pacc = psum("pacc", [128, 512], F32); PACC = Buf("ps_pacc")
        pout = psum("pout", [128, 512], F32); POUT = Buf("ps_pout")

        def load_w(w_ap, k0, nk, n0, ncols):
            t, b = wslots.next()
            view = t[:, 0:nk * ncols].rearrange("p (k n) -> p k n", n=ncols)
            src = w_ap[k0 * 128:(k0 + nk) * 128, n0:n0 + ncols].rearrange("(k p) n -> p k n", p=128)
            P.dma("pool", view, src, writes=[b], group=b.name)
            return view, b

        def ttiles(n):
            r = []
            s = 0
            while s < n:
                l = min(128, n - s); r.append((s, l)); s += l
            return r

        def ftiles(t0, n):
            r = []
            s = 0
            while s < n:
                l = min(512, n - s); r.append((t0 + s, l)); s += l
            return r

        def linear_fm(actT, AB, nk, w_ap, n0, ncols, toks, evac, colblk=512):
            for nb in range(0, ncols, colblk):
                cb = min(colblk, ncols - nb)
                wv, wb = load_w(w_ap, 0, nk, n0 + nb, cb)
                for cc in range(0, cb, 128):
                    cw = min(128, cb - cc)
                    for (t0, l) in toks:
                        ps, PB = pf_pool.next()
                        for k in range(nk):
                            P.op("pe", lambda e, k=k: e.matmul(ps[0:cw, 0:l], wv[:, k, cc:cc + cw], actT[:, k, t0:t0 + l],
                                                                  start=(k == 0), stop=(k == nk - 1)),
                                 reads=[wb, AB], writes=[PB])
                        evac(ps, PB, (nb + cc) // 128, t0, l, cw)

        def linear_tm(actT, AB, nk, w_ap, n0, ncols, toks, evac, colblk=512):
            for nb in range(0, ncols, colblk):
                cb = min(colblk, ncols - nb)
                wv, wb = load_w(w_ap, 0, nk, n0 + nb, cb)
                for (t0, l) in toks:
                    ps, PB = pf_pool.next()
                    for k in range(nk):
                        P.op("pe", lambda e, k=k: e.matmul(ps[0:l, 0:cb], actT[:, k, t0:t0 + l], wv[:, k, 0:cb],
                                                              start=(k == 0), stop=(k == nk - 1)),
                             reads=[wb, AB], writes=[PB])
                    evac(ps, PB, nb, cb, t0, l)

        def transpose_to(dst_fn, DB, src, SBf, l, nchunks, evac_fn=None):
            for g0 in range(0, nchunks, 8):
                g1 = min(nchunks, g0 + 8)
                pb, PBb = pb_pool.next()
                for c in range(g0, g1):
                    P.op("pe", lambda e, c=c: e.transpose(pb[:, (c - g0) * 128:(c - g0) * 128 + l], src[0:l, c * 128:(c + 1) * 128], ident_b[0:l, 0:l]),
                         reads=[SBf, CONST], writes=[PBb])
                for c in range(g0, g1):
                    if evac_fn is not None:
                        evac_fn(c, pb[:, (c - g0) * 128:(c - g0) * 128 + l], PBb)
                    else:
                        P.op("act", lambda e, c=c: e.activation(out=dst_fn(c), in_=pb[:, (c - g0) * 128:(c - g0) * 128 + l], func=AF.Copy),
                             reads=[PBb], writes=[DB])

        def rstd_of(x_t, XB, l, scale_dim):
            sm, SM = sm_pool.next()
            junk, JB = xb_pool.next()
            P.op("act", lambda e: e.activation(out=junk[0:l, 0:scale_dim], in_=x_t[0:l, 0:scale_dim], func=AF.Square, accum_out=sm[0:l, 0:1]),
                 reads=[XB], writes=[JB, SM])
            P.op("act", lambda e: e.activation(out=sm[0:l, 1:2], in_=sm[0:l, 0:1], func=AF.Ln, scale=1.0 / scale_dim, bias=epsT[0:l, 0:1]),
                 reads=[SM, CONST], writes=[SM])
            P.op("act", lambda e: e.activation(out=sm[0:l, 2:3], in_=sm[0:l, 1:2], func=AF.Exp, scale=-0.5),
                 reads=[SM], writes=[SM])
            return sm, SM

        epsT = sb("epsT", [128, 1], F32)
        for (dst, src) in ((ident_f, c_ident),):
            P.dma("sp", dst[:, :], src[:, :], writes=[CONST], group="const")
        for (dst, src, shp) in ((ident_b, c_ident, 128), (m1_b, c_m1, 128), (uincl_b, c_uincl, 128), (lstr_b, c_lstr, 128)):
            P.dma("pool", dst[:, :], src[:, :], writes=[CONST], group="const")
        P.dma("pool", am_b[:, :], c_am[:, :], writes=[CONST], group="const")
        P.dma("pool", ams_b[:, :], c_ams[:, :], writes=[CONST], group="const")
        P.op("dve", lambda e: e.memset(ones_f[:, :], 1.0), writes=[CONST])
        P.op("dve", lambda e: e.memset(ones_b[:, :], 1.0), writes=[CONST])
        P.op("dve", lambda e: e.memset(epsT[:, :], EPS), writes=[CONST])

        scT = sb("scT", [128, KC * 2], BF16)
        SC = Buf("scT")
        for g in range(2):
            xs, XB = xs_pool.next()
            P.dma("sp", xs[0:KC, 0:128], c2[g, :].rearrange("(k p) -> k p", p=128), writes=[XB], group="xs")
            P.op("act", lambda e: e.activation(out=xs[0:KC, 128:256], in_=xs[0:KC, 0:128], func=AF.Silu), reads=[XB], writes=[XB])
            ps, PB = pf_pool.next()
            P.op("pe", lambda e: e.transpose(ps[:, 0:KC], xs[0:KC, 128:256], ident_f[0:KC, 0:KC]), reads=[XB, CONST], writes=[PB])
            scv = scT[:, :].rearrange("p (k g) -> p k g", g=2)
            P.op("dve", lambda e, g=g: e.tensor_copy(out=scv[:, :, g], in_=ps[:, 0:KC]), reads=[PB], writes=[SC])
        scv = scT[:, :].rearrange("p (k g) -> p k g", g=2)
        ADAD = Buf("ada_d", dram=True)
        for i in range(2):
            def ev_ada(ps, PB, nb, cb, t0, l, i=i):
                st, SB_ = st_pool.next()
                xs, XB = xs_pool.next()
                P.dma("sp", xs[0:2, 0:cb], b_ada[i, nb:nb + cb].partition_broadcast(2), writes=[XB], group="xs")
                P.op("dve", lambda e: e.tensor_tensor(out=st[0:2, 0:cb], in0=ps[0:2, 0:cb], in1=xs[0:2, 0:cb], op=ALU.add),
                     reads=[PB, XB], writes=[SB_])
                P.dma("sp", ada_d[i, :, nb:nb + cb], st[0:2, 0:cb], reads=[SB_], writes=[ADAD], group="ada_d")
            linear_tm(scv, SC, KC, w_ada[i], 0, 6 * D, [(0, 2)], ev_ada)
        adav = adaT[:, :].rearrange("p (i g j k) -> p i g j k", i=2, g=2, j=6)
        for i in range(2):
            for g in range(2):
                xs, XB = xs_pool.next()
                P.dma("sp", xs[0:6 * KC, 0:128], ada_d[i, g, :].rearrange("(jk p) -> jk p", p=128), reads=[ADAD], writes=[XB], group="xs")
                ps, PB = pf_pool.next()
                P.op("pe", lambda e: e.transpose(ps[:, 0:6 * KC], xs[0:6 * KC, 0:128], ident_f[0:6 * KC, 0:6 * KC]), reads=[XB, CONST], writes=[PB])
                P.op("dve", lambda e, i=i, g=g: e.tensor_copy(out=adaT[:, (i * 2 + g) * 6 * KC:(i * 2 + g + 1) * 6 * KC], in_=ps[:, 0:6 * KC]),
                     reads=[PB], writes=[ADA])
        xs, XB = xs_pool.next()
        NG = 2 * 4 * KC
        P.dma("sp", xs[0:NG, 0:128], g_norm.rearrange("i j (k p) -> (i j k) p", p=128), writes=[XB], group="xs")
        ps, PB = pf_pool.next()
        P.op("pe", lambda e: e.transpose(ps[:, 0:NG], xs[0:NG, 0:128], ident_f[0:NG, 0:NG]), reads=[XB, CONST], writes=[PB])
        P.op("dve", lambda e: e.tensor_copy(out=gnT[:, 0:NG], in_=ps[:, 0:NG]), reads=[PB], writes=[ADA])
        gnv = gnT[:, 0:NG].rearrange("p (i j k) -> p i j k", i=2, j=4)
        modv = modA[:, :].rearrange("p (i s g k) -> p i s g k", i=2, s=2, g=2)
        for i in range(2):
            for s in range(2):
                for g in range(2):
                    P.op("dve", lambda e, i=i, s=s, g=g: e.scalar_tensor_tensor(
                        out=modv[:, i, s, g, :], in0=adav[:, i, g, 3 * s + 1, :], scalar=1.0, in1=gnv[:, i, 2 * s, :],
                        op0=ALU.add, op1=ALU.mult), reads=[ADA], writes=[ADA])

        ckpt(P, "ada")
        XD = Buf("x_d", dram=True); OD = Buf("o_d", dram=True)

        def prenorm(i, s, src_ap, SRC, ntok, dstT, DB, grp_of_tile, col0=0):
            for (t0, l) in ttiles(ntok):
                g = grp_of_tile(t0)
                xs, XB = xs_pool.next()
                P.dma("sp", xs[0:l, :], src_ap[t0:t0 + l, :], reads=[SRC], writes=[XB], group="xs")
                sm, SM = rstd_of(xs, XB, l, D)
                xb, XBB = xb_pool.next()
                P.op("dve", lambda e: e.tensor_scalar(out=xb[0:l, :], in0=xs[0:l, :], scalar1=sm[0:l, 2:3], scalar2=None, op0=ALU.mult),
                     reads=[XB, SM], writes=[XBB])

                def ev(c, pin, PBb, g=g, t0=t0, l=l):
                    P.op("act", lambda e: e.activation(out=dstT[:, c, col0 + t0:col0 + t0 + l], in_=pin, func=AF.Identity,
                                                       scale=modv[:, i, s, g, c:c + 1], bias=adav[:, i, g, 3 * s, c:c + 1]),
                         reads=[PBb, ADA], writes=[DB])
                transpose_to(None, DB, xb, XBB, l, KC, evac_fn=ev)

        def build_gb(i, s, g):
            xs, XB = xs_pool.next()
            P.dma("sp", gb_t[:, :], ada_d[i, g, (3 * s + 2) * D:(3 * s + 3) * D].partition_broadcast(128), reads=[ADAD], writes=[GB])
            P.dma("sp", xs[:, :], g_norm[i, 2 * s + 1, :].partition_broadcast(128), writes=[XB])
            P.op("dve", lambda e: e.tensor_tensor(out=gb_t[:, :], in0=gb_t[:, :], in1=xs[:, :], op=ALU.mult), reads=[XB, GB], writes=[GB])

        def post(i, s, xsrc_ap, XSRC, final_out=None):
            cur_g = None
            for (t0, l) in ttiles(T):
                g = 0 if t0 < TO else 1
                if g != cur_g:
                    build_gb(i, s, g)
                    cur_g = g
                os_, OB = xs_pool.next()
                P.dma("sp", os_[0:l, :], o_d[t0:t0 + l, :], reads=[OD], writes=[OB])
                sm, SM = rstd_of(os_, OB, l, D)
                P.op("dve", lambda e: e.scalar_tensor_tensor(out=os_[0:l, :], in0=os_[0:l, :], scalar=sm[0:l, 2:3], in1=gb_t[0:l, :],
                                                             op0=ALU.mult, op1=ALU.mult), reads=[OB, SM, GB], writes=[OB])
                xs, XB = xs_pool.next()
                P.dma("sp", xs[0:l, :], xsrc_ap[t0:t0 + l, :], reads=[XSRC], writes=[XB])
                P.op("dve", lambda e: e.tensor_tensor(out=xs[0:l, :], in0=xs[0:l, :], in1=os_[0:l, :], op=ALU.add), reads=[OB, XB], writes=[XB])
                if final_out is not None:
                    P.dma("sp", final_out[t0:t0 + l, :], xs[0:l, :], reads=[XB], writes=[OUTB])
                else:
                    P.dma("sp", x_d[t0:t0 + l, :], xs[0:l, :], reads=[XB], writes=[XD])

        OUTB = Buf("out", dram=True)

        def out_proj(w_ap, nk, actT, AB):
            def ev(ps, PB, nb, cb, t0, l):
                st, SB_ = st_pool.next()
                P.op("act", lambda e: e.activation(out=st[0:l, 0:cb], in_=ps[0:l, 0:cb], func=AF.Copy), reads=[PB], writes=[SB_])
                P.dma("sp", o_d[t0:t0 + l, nb:nb + cb], st[0:l, 0:cb], reads=[SB_], writes=[OD], group="o_d")
            linear_tm(actT, AB, nk, w_ap, 0, D, ttiles(T), ev)

        def ffn(i, xsrc_ap, XSRC, final_out=None):
            prenorm(i, 1, xsrc_ap, XSRC, T, hT, B1, lambda t0: 0 if t0 < TO else 1)
            halves = [(0, min(512, T))] + ([(512, T - 512)] if T > 512 else [])
            for (h0, hn) in halves:
                hidT = buf2[:, 0:FC * hn].rearrange("p (k t) -> p k t", t=hn)
                for nb in range(0, DFF, 512):
                    cb = min(512, DFF - nb)
                    gv, gb_ = load_w(w_ffn_in[i], 0, KC, nb, cb)
                    uv, ub_ = load_w(w_ffn_in[i], 0, KC, DFF + nb, cb)
                    for cc in range(0, cb, 128):
                        j = (nb + cc) // 128
                        for (t0, l) in ftiles(h0, hn):
                            pg, PG = pf_pool.next()
                            for k in range(KC):
                                P.op("pe", lambda e, k=k: e.matmul(pg[:, 0:l], gv[:, k, cc:cc + 128], hT[:, k, t0:t0 + l], start=(k == 0), stop=(k == KC - 1)),
                                     reads=[gb_, B1], writes=[PG])
                            pu, PU = pf_pool.next()
                            for k in range(KC):
                                P.op("pe", lambda e, k=k: e.matmul(pu[:, 0:l], uv[:, k, cc:cc + 128], hT[:, k, t0:t0 + l], start=(k == 0), stop=(k == KC - 1)),
                                     reads=[ub_, B1], writes=[PU])
                            st, SB_ = st_pool.next()
                            P.op("act", lambda e: e.activation(out=st[:, 0:l], in_=pg[:, 0:l], func=AF.Silu), reads=[PG], writes=[SB_])
                            P.op("dve", lambda e: e.tensor_tensor(out=hidT[:, j, t0 - h0:t0 - h0 + l], in0=pu[:, 0:l], in1=st[:, 0:l], op=ALU.mult),
                                 reads=[PU, SB_], writes=[B2])
                toks = ttiles(hn)
                for nb in range(0, D, 128):
                    wv, wb = load_w(w_ffn_out[i], 0, FC, nb, 128)
                    for (t0, l) in toks:
                        ps, PB = pf_pool.next()
                        for k in range(FC):
                            P.op("pe", lambda e, k=k: e.matmul(ps[0:l, 0:128], hidT[:, k, t0:t0 + l], wv[:, k, :], start=(k == 0), stop=(k == FC - 1)),
                                 reads=[wb, B2], writes=[PB])
                        st, SB_ = st_pool.next()
                        P.op("act", lambda e: e.activation(out=st[0:l, 0:128], in_=ps[0:l, 0:128], func=AF.Copy), reads=[PB], writes=[SB_])
                        P.dma("sp", o_d[h0 + t0:h0 + t0 + l, nb:nb + 128], st[0:l, 0:128], reads=[SB_], writes=[OD], group="o_d")
            post(i, 1, xsrc_ap, XSRC, final_out=final_out)

        XIN = Buf("xin", dram=True)
        KD = Buf("K_d", dram=True); VD = Buf("V_d", dram=True); GD = Buf("G_d", dram=True)
        o1 = QK; o2 = 2 * QK; o3 = o2 + VT; o4 = o3 + VT
        push_scope()
        NR = max(TC, T)
        girows = sb("girows", [MH, NR], F32); gfrows = sb("gfrows", [MH, NR], F32); GR = Buf("grows")
        brows = sb("brows", [MH, NR], F32)
        arows = girows
        werows = gfrows
        ebrows_t = sb("ebrows", [MH, 128], F32)
        gsm = sb("gsm", [MH, 64], F32)
        bg = sb("bg", [MH, 4], F32)
        fl = sb("fl", [MH, 2], F32)
        ghb = sb("ghb", [128, VT], BF16); GHB = Buf("ghb")
        P.dma("sp", bg[:, 0:2], b_gate[:, :], writes=[GR], group="small")
        P.dma("sp", fl[:, :], flags[:, :], writes=[GR], group="small")
        P.dma("pool", ghb[:, :], g_head.partition_broadcast(128), writes=[GHB], group="ghb")
        P.op("dve", lambda e: e.tensor_scalar(out=bg[:, 2:3], in0=bg[:, 1:2], scalar1=-1.0, scalar2=None, op0=ALU.mult), reads=[GR], writes=[GR])

        def ev_to_dram_bf(dst_ap, DSTB, row0, func=AF.Copy, mul_ap=None, MB=None):
            def ev(ps, PB, nb, cb, t0, l):
                st, SB_ = sbf_pool.next()
                P.op("act", lambda e: e.activation(out=st[0:l, 0:cb], in_=ps[0:l, 0:cb], func=func), reads=[PB], writes=[SB_])
                if mul_ap is not None:
                    P.op("dve", lambda e: e.tensor_tensor(out=st[0:l, 0:cb], in0=st[0:l, 0:cb], in1=mul_ap[0:l, nb:nb + cb], op=ALU.mult),
                         reads=[SB_, MB], writes=[SB_])
                P.dma("sp", dst_ap[row0 + t0:row0 + t0 + l, nb:nb + cb], st[0:l, 0:cb], reads=[SB_], writes=[DSTB], group=DSTB.name)
            return ev

        def ev_gates(col_base):
            def ev(ps, PB, ci, t0, l, cw):
                pass
            return ev

        def gates_proj(actT, AB, toks, col0):
            wv, wb = load_w(w_in_a, 0, KC, o4, 2 * MH)
            for (t0, l) in toks:
                for (which, dst) in ((0, girows), (1, gfrows)):
                    ps, PB = pf_pool.next()
                    for k in range(KC):
                        P.op("pe", lambda e, k=k: e.matmul(ps[0:MH, 0:l], wv[:, k, which * MH:(which + 1) * MH], actT[:, k, t0:t0 + l],
                                                              start=(k == 0), stop=(k == KC - 1)), reads=[wb, AB], writes=[PB])
                    P.op("act", lambda e, dst=dst: e.activation(out=dst[:, col0 + t0:col0 + t0 + l], in_=ps[0:MH, 0:l], func=AF.Copy),
                         reads=[PB], writes=[GR])

        Cn_t = sb("Cn_t", [128, MH * 257], F32); CNB_ = Buf("Cn_t")
        Cnv_ = Cn_t[:, :].rearrange("p (h v) -> p h v", v=257)
        mrow_p = sb("mrow_p", [MH, 4], F32); mrow_s = sb("mrow_s", [MH, 4], F32); MR = Buf("mrow")
        P.op("pool", lambda e: e.memset(Cn_t[:, :], 0.0), writes=[CNB_])
        P.op("dve", lambda e: e.memset(mrow_p[:, :], 0.0), writes=[MR])
        P.dma("sp", mrow_s[:, 0:1], st_m[:, :], writes=[MR])

        vaug_pool = Rot([(sb("vaug%d" % i, [128, MH * 257], BF16), Buf("vaug%d" % i)) for i in range(1)])
        for (vt, VB) in vaug_pool.items:
            P.op("pool", lambda e, vt=vt: e.memset(vt[:, :], 1.0), writes=[VB])
        ktm_pool = Rot([(sb("ktm%d" % i, [128, QK], BF16), Buf("ktm%d" % i)) for i in range(1)])
        gg_pool = Rot([(sb("gg%d" % i, [128, VT], BF16), Buf("gg%d" % i)) for i in range(1)])
        hs_pool = Rot([(sb("hs%d" % i, [128, VT], BF16), Buf("hs%d" % i)) for i in range(1)])
        wetm_pool = Rot([(sb("wetm%d" % i, [128, 64], F32), Buf("wetm%d" % i)) for i in range(2)])
        decbc_pool = Rot([(sb("decbc%d" % i, [128, MH], F32), Buf("decbc%d" % i)) for i in range(2)])
        cdb_pool = Rot([(sb("cdb%d" % i, [128, 257], BF16), Buf("cdb%d" % i)) for i in range(2)])
        vw_pool = Rot([(sb("vw%d" % i, [128, 257], BF16), Buf("vw%d" % i)) for i in range(2)])
        sd_pool = Rot([(sb("sd%d" % i, [128, 128], BF16), Buf("sd%d" % i)) for i in range(2)])
        aT = hT


        def gate_math(n, is_prefix):
            P.op("dve", lambda e: e.tensor_scalar(out=girows[:, 0:n], in0=girows[:, 0:n], scalar1=bg[:, 0:1], scalar2=None, op0=ALU.add), reads=[GR], writes=[GR])
            if is_prefix:
                P.op("dve", lambda e: e.tensor_scalar(out=girows[:, 0:n], in0=girows[:, 0:n], scalar1=fl[:, 1:2], scalar2=None, op0=ALU.add), reads=[GR], writes=[GR])
            P.op("act", lambda e: e.activation(out=gfrows[:, 0:n], in_=gfrows[:, 0:n], func=AF.Exp, scale=-1.0, bias=bg[:, 2:3]), reads=[GR], writes=[GR])
            P.op("act", lambda e: e.activation(out=gfrows[:, 0:n], in_=gfrows[:, 0:n], func=AF.Ln, bias=ones_f[0:MH, 0:1]), reads=[GR, CONST], writes=[GR])
            P.op("dve", lambda e: e.tensor_scalar(out=gfrows[:, 0:n], in0=gfrows[:, 0:n], scalar1=-1.0, scalar2=None, op0=ALU.mult), reads=[GR], writes=[GR])
            if is_prefix:
                P.op("dve", lambda e: e.tensor_scalar(out=gfrows[:, 0:n], in0=gfrows[:, 0:n], scalar1=fl[:, 0:1], scalar2=None, op0=ALU.mult), reads=[GR], writes=[GR])
            r = 0
            while r < n:
                Lc = min(128, n - r)
                P.op("dve", lambda e, r=r, Lc=Lc: e.tensor_tensor_scan(out=brows[:, r:r + Lc], data0=ones_f[0:MH, 0:Lc], data1=gfrows[:, r:r + Lc],
                                                                        initial=0.0, op0=ALU.mult, op1=ALU.add), reads=[GR, CONST], writes=[GR])
                r += Lc
            P.op("dve", lambda e: e.tensor_tensor(out=arows[:, 0:n], in0=girows[:, 0:n], in1=brows[:, 0:n], op=ALU.subtract), reads=[GR], writes=[GR])

        def mlstm_chunk(c0, r0, L, kind, Cn, CNB, mrow):
            gs, GS = sm_pool.next()
            P.op("dve", lambda e: e.tensor_reduce(out=gs[0:MH, 0:1], in_=arows[:, r0:r0 + L], axis=AX.X, op=ALU.max), reads=[GR], writes=[GS])
            P.op("dve", lambda e: e.tensor_tensor(out=gs[0:MH, 1:2], in0=gs[0:MH, 0:1], in1=mrow[:, 0:1], op=ALU.max), reads=[GS, MR], writes=[GS])
            P.op("dve", lambda e: e.tensor_scalar(out=gs[0:MH, 2:3], in0=gs[0:MH, 1:2], scalar1=-1.0, scalar2=None, op0=ALU.mult), reads=[GS], writes=[GS])
            P.op("act", lambda e: e.activation(out=gs[0:MH, 3:4], in_=mrow[:, 0:1], func=AF.Exp, bias=gs[0:MH, 2:3]), reads=[GS, MR], writes=[GS])
            P.op("act", lambda e: e.activation(out=werows[0:MH, r0:r0 + L], in_=arows[:, r0:r0 + L], func=AF.Exp, bias=gs[0:MH, 2:3]), reads=[GS, GR], writes=[GR])
            if kind != "pre":
                P.op("act", lambda e: e.activation(out=ebrows_t[:, 0:L], in_=brows[:, r0:r0 + L], func=AF.Exp, scale=-1.0,
                                                   bias=gs[0:MH, 2:3]), reads=[GS, GR], writes=[GR])
            P.op("dve", lambda e: e.tensor_tensor(out=mrow[:, 0:1], in0=brows[:, r0 + L - 1:r0 + L], in1=gs[0:MH, 1:2], op=ALU.add), reads=[GS, GR, MR], writes=[MR])
            ps, PB = pf_pool.next()
            P.op("pe", lambda e: e.transpose(ps[0:L, 0:MH], werows[0:MH, r0:r0 + L], ident_f[0:MH, 0:MH]), reads=[GR, CONST], writes=[PB])
            if kind != "pre":
                P.op("pe", lambda e: e.transpose(ps[0:L, 32:32 + MH], ebrows_t[0:MH, 0:L], ident_f[0:MH, 0:MH]), reads=[GR, CONST], writes=[PB])
            else:
                P.op("pe", lambda e: e.transpose(ps[0:L, 32:32 + MH], werows[0:MH, r0:r0 + L], ident_f[0:MH, 0:MH]), reads=[GR, CONST], writes=[PB])
            we, WE = wetm_pool.next()
            P.op("dve", lambda e: e.tensor_copy(out=we[0:L, :], in_=ps[0:L, 0:64]), reads=[PB], writes=[WE])
            P.op("dve", lambda e: e.tensor_scalar(out=gs[0:MH, 8:8 + MH], in0=ident_f[0:MH, 0:MH], scalar1=gs[0:MH, 3:4], scalar2=None, op0=ALU.mult),
                 reads=[GS, CONST], writes=[GS])
            ps2, PB2 = pf_pool.next()
            P.op("pe", lambda e: e.matmul(ps2[:, 0:MH], ones_f[0:MH, 0:128], gs[0:MH, 8:8 + MH], start=True, stop=True), reads=[GS, CONST], writes=[PB2])
            db, DBB = decbc_pool.next()
            P.op("dve", lambda e: e.tensor_copy(out=db[:, :], in_=ps2[:, 0:MH]), reads=[PB2], writes=[DBB])
            va, VA = vaug_pool.next()
            vav = va[:, :].rearrange("p (h v) -> p h v", v=257)
            P.dma("sp", vav[0:L, :, 0:256], V_d[c0:c0 + L, :].rearrange("t (h v) -> t h v", v=256), reads=[VD], writes=[VA], group=VA.name)
            kt, KTB = ktm_pool.next()
            P.dma("sp", kt[0:L, :], K_d[c0:c0 + L, :], reads=[KD], writes=[KTB], group=KTB.name)
            if kind != "pre":
                gg, GGB = gg_pool.next()
                P.dma("sp", gg[0:L, :], G_d[c0 - TC:c0 - TC + L, :], reads=[GD], writes=[GGB], group=GGB.name)
                hs, HSB = hs_pool.next()
                q0 = c0 - TC
            for h in range(MH):
                P.op("dve", lambda e, h=h: e.tensor_scalar(out=Cn[:, h, :], in0=Cn[:, h, :], scalar1=db[:, h:h + 1], scalar2=None, op0=ALU.mult),
                     reads=[CNB, DBB], writes=[CNB])
                if kind != "pre":
                    cdb, CDB = cdb_pool.next()
                    P.op("act", lambda e, h=h: e.activation(out=cdb[:, :], in_=Cn[:, h, :], func=AF.Copy), reads=[CNB], writes=[CDB])
                    pS, PS_ = pf_pool.next()
                    P.op("pe", lambda e, h=h: e.matmul(pS[0:L, 0:L], kT[:, h, q0:q0 + L], qT[:, h, q0:q0 + L], start=True, stop=True), reads=[B2], writes=[PS_])
                    sd, SDB = sd_pool.next()
                    P.op("dve", lambda e, h=h: e.scalar_tensor_tensor(out=sd[0:L, 0:L], in0=pS[0:L, 0:L], scalar=we[0:L, h:h + 1], in1=m1_b[0:L, 0:L],
                                                                     op0=ALU.mult, op1=ALU.mult), reads=[PS_, WE, CONST], writes=[SDB])
                    pO, PO_ = pf_pool.next()
                    P.op("pe", lambda e, h=h: e.matmul(pO[0:L, 0:257], qT[:, h, q0:q0 + L], cdb[:, :], start=True, stop=False), reads=[B2, CDB], writes=[PO_])
                    P.op("pe", lambda e, h=h: e.matmul(pO[0:L, 0:257], sd[0:L, 0:L], vav[0:L, h, :], start=False, stop=True), reads=[SDB, VA], writes=[PO_])
                vw, VWB = vw_pool.next()
                P.op("pool", lambda e, h=h: e.tensor_scalar(out=vw[0:L, :], in0=vav[0:L, h, :], scalar1=we[0:L, h:h + 1], scalar2=None, op0=ALU.mult),
                     reads=[VA, WE], writes=[VWB])
                pU, PU_ = pf_pool.next()
                P.op("pe", lambda e, h=h: e.matmul(pU[:, 0:257], kt[0:L, h * 128:(h + 1) * 128], vw[0:L, :], start=True, stop=True), reads=[KTB, VWB], writes=[PU_])
                P.op("dve", lambda e, h=h: e.tensor_tensor(out=Cn[:, h, :], in0=Cn[:, h, :], in1=pU[:, 0:257], op=ALU.add), reads=[CNB, PU_], writes=[CNB])
                if kind != "pre":
                    sm, SM = sm_pool.next()
                    P.op("act", lambda e: e.activation(out=sm[0:L, 0:1], in_=pO[0:L, 256:257], func=AF.Abs), reads=[PO_], writes=[SM])
                    P.op("dve", lambda e, h=h: e.tensor_tensor(out=sm[0:L, 0:1], in0=sm[0:L, 0:1], in1=we[0:L, 32 + h:33 + h], op=ALU.max), reads=[SM, WE], writes=[SM])
                    P.op("dve", lambda e: e.reciprocal(out=sm[0:L, 1:2], in_=sm[0:L, 0:1]), reads=[SM], writes=[SM])
                    junk, JB = sbf_pool.next()
                    P.op("act", lambda e: e.activation(out=junk[0:L, 0:256], in_=pO[0:L, 0:256], func=AF.Square, scale=sm[0:L, 1:2], accum_out=sm[0:L, 2:3]),
                         reads=[PO_, SM], writes=[JB, SM])
                    P.op("act", lambda e: e.activation(out=sm[0:L, 3:4], in_=sm[0:L, 2:3], func=AF.Ln, scale=1.0 / DV, bias=epsT[0:L, 0:1]), reads=[SM, CONST], writes=[SM])
                    P.op("act", lambda e: e.activation(out=sm[0:L, 4:5], in_=sm[0:L, 3:4], func=AF.Exp, scale=-0.5), reads=[SM], writes=[SM])
                    P.op("dve", lambda e: e.tensor_tensor(out=sm[0:L, 5:6], in0=sm[0:L, 4:5], in1=sm[0:L, 1:2], op=ALU.mult), reads=[SM], writes=[SM])
                    P.op("dve", lambda e, h=h: e.scalar_tensor_tensor(out=hs[0:L, h * 256:(h + 1) * 256], in0=pO[0:L, 0:256], scalar=sm[0:L, 5:6],
                                                                     in1=gg[0:L, h * 256:(h + 1) * 256], op0=ALU.mult, op1=ALU.mult),
                         reads=[PO_, SM, GGB], writes=[HSB])
            if kind != "pre":
                transpose_to(lambda c: aT[:, c, q0:q0 + L], B1, hs, HSB, L, KC)


        prenorm(0, 0, x_pre, XIN, TC, hT, B1, lambda t0: 0)
        hTp = hT
        linear_tm(hTp, B1, KC, w_in_a, o1, QK, ttiles(TC), ev_to_dram_bf(K_d, KD, 0))
        linear_tm(hTp, B1, KC, w_in_a, o2, VT, ttiles(TC), ev_to_dram_bf(V_d, VD, 0))
        gates_proj(hTp, B1, ftiles(0, TC), 0)
        gate_math(TC, True)
        for c in range(TC // 128):
            mlstm_chunk(c * 128, c * 128, 128, "pre", Cnv_, CNB_, mrow_p)
        prenorm(0, 0, x_own, XIN, T, hT, B1, lambda t0: 0 if t0 < TO else 1)
        qT = buf2[:, 0:MH * T].rearrange("p (h t) -> p h t", t=T)
        kT = buf2[:, MH * T:2 * MH * T].rearrange("p (h t) -> p h t", t=T)

        def ev_q(ps, PB, ci, t0, l, cw):
            P.op("act", lambda e: e.activation(out=qT[:, ci, t0:t0 + l], in_=ps[:, 0:l], func=AF.Copy, scale=QS), reads=[PB], writes=[B2])

        def ev_k(ps, PB, ci, t0, l, cw):
            P.op("act", lambda e: e.activation(out=kT[:, ci, t0:t0 + l], in_=ps[:, 0:l], func=AF.Copy), reads=[PB], writes=[B2])
        linear_fm(hT, B1, KC, w_in_a, 0, QK, ftiles(0, T), ev_q)
        linear_fm(hT, B1, KC, w_in_a, o1, QK, ftiles(0, T), ev_k)
        linear_tm(hT, B1, KC, w_in_a, o1, QK, ttiles(T), ev_to_dram_bf(K_d, KD, TC))
        linear_tm(hT, B1, KC, w_in_a, o2, VT, ttiles(T), ev_to_dram_bf(V_d, VD, TC))
        linear_tm(hT, B1, KC, w_in_a, o3, VT, ttiles(T), ev_to_dram_bf(G_d, GD, 0, func=AF.Sigmoid, mul_ap=ghb, MB=GHB))
        gates_proj(hT, B1, ftiles(0, T), 0)
        ckpt(P, "l0proj")
        gate_math(T, False)
        chunks = [(TC + c * 128, c * 128, 128, "own") for c in range(NT)] + [(TC + TO, TO, TS, "smp")]

        def state_out(mrow, Co, no, mo):
            P.dma("sp", Co.rearrange("h d v -> d h v"), Cnv_[:, :, 0:256], reads=[CNB_], writes=[OUTB])
            sm, SM = sm_pool.next()
            P.op("dve", lambda e: e.tensor_copy(out=sm[:, 0:MH], in_=Cnv_[:, :, 256]), reads=[CNB_], writes=[SM])
            ps, PB = pf_pool.next()
            P.op("pe", lambda e: e.transpose(ps[0:MH, 0:128], sm[:, 0:MH], ident_f[:, :]), reads=[SM, CONST], writes=[PB])
            st, SB_ = st_pool.next()
            P.op("dve", lambda e: e.tensor_copy(out=st[0:MH, 0:128], in_=ps[0:MH, 0:128]), reads=[PB], writes=[SB_])
            P.dma("sp", no[:, :], st[0:MH, 0:128], reads=[SB_], writes=[OUTB])
            P.dma("sp", mo[:, :], mrow[:, 0:1], reads=[MR], writes=[OUTB])

        for (c0, r0, L, kind) in chunks:
            if kind == "smp":
                state_out(mrow_p, C_p, n_p, m_p)
                P.dma("sp", Cnv_[:, :, 0:256], st_C.rearrange("h d v -> d h v"), writes=[CNB_])
                xs, XB = xs_pool.next()
                P.dma("sp", xs[0:MH, 0:128], st_n[:, :], writes=[XB])
                ps, PB = pf_pool.next()
                P.op("pe", lambda e: e.transpose(ps[:, 0:MH], xs[0:MH, 0:128], ident_f[0:MH, 0:MH]), reads=[XB, CONST], writes=[PB])
                P.op("dve", lambda e: e.tensor_copy(out=Cnv_[:, :, 256], in_=ps[:, 0:MH]), reads=[PB], writes=[CNB_])
                mlstm_chunk(c0, r0, L, kind, Cnv_, CNB_, mrow_s)
                state_out(mrow_s, C_s, n_s, m_s)
            else:
                mlstm_chunk(c0, r0, L, kind, Cnv_, CNB_, mrow_p)
        pop_scope()
        ckpt(P, "l0mix")
        out_proj(w_out_a, KC, aT, B1)
        ckpt(P, "l0op")
        post(0, 0, x_own, XIN)
        ckpt(P, "l0post")
        if not os.environ.get("KSKIPFFN"):
            ffn(0, x_d, XD)
        ckpt(P, "l0")

        grp = lambda t0: 0 if t0 < TO else 1
        prenorm(1, 0, x_d, XD, T, hT, B1, grp)
        fT = ftiles(0, TO) + [(TO, TS)]
        tT = ttiles(TO) + [(TO, TS)]
        qT1 = buf2[:, 0:H * T].rearrange("p (h t) -> p h t", t=T)
        push_scope()
        kTs = sb("kTs", [128, H * TS], BF16); KTS = Buf("kTs")
        kTsv = kTs[:, :].rearrange("p (h t) -> p h t", t=TS)
        Vs = sb("Vs", [TS, D], BF16); VS = Buf("Vs")
        BKI = [Buf("bk_in%d" % i, dram=True) for i in range(nkc)]; BKO = [Buf("bk_out%d" % i, dram=True) for i in range(nkc)]
        BVI = [Buf("bv_in%d" % i, dram=True) for i in range(nkc)]; BVO = [Buf("bv_out%d" % i, dram=True) for i in range(nkc)]

        def ev_q1(ps, PB, ci, t0, l, cw):
            P.op("act", lambda e: e.activation(out=qT1[:, ci, t0:t0 + l], in_=ps[:, 0:l], func=AF.Copy, scale=SB_SCALE), reads=[PB], writes=[B2])

        def ev_k1(ps, PB, ci, t0, l, cw):
            if t0 < TO:
                st, SB_ = sbf_pool.next()
                P.op("act", lambda e: e.activation(out=st[:, 0:l], in_=ps[:, 0:l], func=AF.Copy), reads=[PB], writes=[SB_])
                kc_ = (ci * 128) // KROWS; r_ = ci * 128 - kc_ * KROWS
                P.dma("sp", bk_in[kc_].ap()[r_:r_ + 128, t0:t0 + l], st[:, 0:l], reads=[SB_], writes=[BKI[kc_]])
            else:
                P.op("act", lambda e: e.activation(out=kTsv[:, ci, 0:l], in_=ps[:, 0:l], func=AF.Copy), reads=[PB], writes=[KTS])
        ckpt(P, "l1a")
        linear_fm(hT, B1, KC, w_in_b, 0, D, fT, ev_q1)
        ckpt(P, "l1b")
        linear_fm(hT, B1, KC, w_in_b, D, D, fT, ev_k1)
        ckpt(P, "l1c")

        def ev_ktm(ps, PB, nb, cb, t0, l):
            st, SB_ = st_pool.next()
            P.op("act", lambda e: e.activation(out=st[0:l, 0:cb], in_=ps[0:l, 0:cb], func=AF.Copy), reads=[PB], writes=[SB_])
            if t0 < TO:
                P.dma("sp", k_own[t0:t0 + l, nb:nb + cb], st[0:l, 0:cb], reads=[SB_], writes=[OUTB], group="out")
            else:
                P.dma("sp", k_s[0:l, nb:nb + cb], st[0:l, 0:cb], reads=[SB_], writes=[OUTB], group="out")

        def ev_vtm(ps, PB, nb, cb, t0, l):
            st, SB_ = st_pool.next()
            P.op("act", lambda e: e.activation(out=st[0:l, 0:cb], in_=ps[0:l, 0:cb], func=AF.Copy), reads=[PB], writes=[SB_])
            if t0 < TO:
                P.dma("sp", v_own[t0:t0 + l, nb:nb + cb], st[0:l, 0:cb], reads=[SB_], writes=[OUTB], group="out")
                sb2, SB2 = sbf_pool.next()
                P.op("dve", lambda e: e.tensor_copy(out=sb2[0:l, 0:cb], in_=ps[0:l, 0:cb]), reads=[PB], writes=[SB2])
                vc_ = t0 // VROWS; r_ = t0 - vc_ * VROWS
                P.dma("sp", bv_in[vc_].ap()[r_:r_ + l, nb:nb + cb], sb2[0:l, 0:cb], reads=[SB2], writes=[BVI[vc_]])
            else:
                P.dma("sp", v_s[0:l, nb:nb + cb], st[0:l, 0:cb], reads=[SB_], writes=[OUTB], group="out")
                P.op("dve", lambda e: e.tensor_copy(out=Vs[0:l, nb:nb + cb], in_=ps[0:l, 0:cb]), reads=[PB], writes=[VS])
        linear_tm(hT, B1, KC, w_in_b, D, D, tT, ev_ktm)
        ckpt(P, "l1d")
        linear_tm(hT, B1, KC, w_in_b, 2 * D, D, tT, ev_vtm)

        ckpt(P, "l1proj")
        for i in range(nkc):
            for (tin, tout, BI_, BO_) in ((bk_in[i], bk_out[i], BKI[i], BKO[i]), (bv_in[i], bv_out[i], BVI[i], BVO[i])):
                P.collective(lambda e, tin=tin, tout=tout: e.collective_compute(
                    "AllGather", ALU.bypass, replica_groups=[[0, 1], [2, 3], [4, 5], [6, 7]],
                    ins=[tin.ap().opt()], outs=[tout.ap().opt()]), reads=[BI_], writes=[BO_])
        ckpt(P, "cc")
        bias_bc = sb("bias_bc", [128, H], F32)
        vfl = sb("vfl", [128, 1], F32)
        biasrow = sb("biasrow", [1, H * 8], F32)
        P.dma("sp", bias_bc[:, :], b_sb.partition_broadcast(128), writes=[CONST], group="const")
        P.dma("sp", vfl[:, :], vflag[:, :], writes=[CONST], group="const")
        brv = biasrow[:, :].rearrange("p (h t) -> p h t", t=8)
        for t in range(8):
            P.op("dve", lambda e, t=t: e.tensor_copy(out=brv[0:1, :, t], in_=bias_bc[0:1, :]), reads=[CONST], writes=[CONST])

        NKB = (TC + TO) // 128
        e_pool = Rot([(sb("ef%d" % i, [128, 512], F32), Buf("ef%d" % i)) for i in range(2)])
        zs_pool = Rot([(sb("zs%d" % i, [128, 512], F32), Buf("zs%d" % i)) for i in range(2)])
        sp_pool = Rot([(sb("spb%d" % i, [128, 512], BF16), Buf("spb%d" % i)) for i in range(2)])
        a_pool = Rot([(sb("ab%d" % i, [128, 512], BF16), Buf("ab%d" % i)) for i in range(2)])
        amv = am_b[:, :].rearrange("p (o t) -> p o t", t=512)
        push_scope()
        kth_pool = Rot([(sb("kth%d" % i, [128, TC + TO], BF16), Buf("kth%d" % i)) for i in range(2)])
        vh_pool = Rot([(sb("vh%d" % i, [128, NKB * 128], BF16), Buf("vh%d" % i)) for i in range(2)])

        def sb_block(Zps, ZB, width, jl, first, last, bias_ap, mask_ap, av_fn):
            ef, EF = e_pool.next()
            if bias_ap is not None:
                P.op("act", lambda e: e.activation(out=ef[0:jl, 0:width], in_=Zps[0:jl, 0:width], func=AF.Exp, bias=bias_ap), reads=[ZB, CONST], writes=[EF])
            else:
                P.op("act", lambda e: e.activation(out=ef[0:jl, 0:width], in_=Zps[0:jl, 0:width], func=AF.Exp), reads=[ZB], writes=[EF])
            spb, SPB = sp_pool.next()
            P.op("act", lambda e: e.activation(out=spb[0:jl, 0:width], in_=ef[0:jl, 0:width], func=AF.Ln, bias=ones_f[0:jl, 0:1]), reads=[EF, CONST], writes=[SPB])
            if mask_ap is not None:
                P.op("pool", lambda e: e.tensor_tensor(out=spb[0:jl, 0:width], in0=spb[0:jl, 0:width], in1=mask_ap, op=ALU.mult), reads=[SPB, CONST], writes=[SPB])
            P.op("pe", lambda e: e.matmul(pacc[0:128, 0:width], uincl_b[0:jl, 0:128], spb[0:jl, 0:width], start=first, stop=True), reads=[SPB, CONST], writes=[PACC])
            zs, ZS = zs_pool.next()
            if bias_ap is not None:
                P.op("dve", lambda e: e.tensor_scalar(out=zs[0:jl, 0:width], in0=Zps[0:jl, 0:width], scalar1=bias_ap, scalar2=None, op0=ALU.add), reads=[ZB, CONST], writes=[ZS])
            else:
                P.op("dve", lambda e: e.tensor_copy(out=zs[0:jl, 0:width], in_=Zps[0:jl, 0:width]), reads=[ZB], writes=[ZS])
            P.op("dve", lambda e: e.tensor_tensor(out=zs[0:jl, 0:width], in0=zs[0:jl, 0:width], in1=pacc[0:jl, 0:width], op=ALU.subtract), reads=[ZS, PACC], writes=[ZS])
            ab, AB_ = a_pool.next()
            P.op("act", lambda e: e.activation(out=ab[0:jl, 0:width], in_=zs[0:jl, 0:width], func=AF.Exp), reads=[ZS], writes=[AB_])
            if mask_ap is not None:
                P.op("pool", lambda e: e.tensor_tensor(out=ab[0:jl, 0:width], in0=ab[0:jl, 0:width], in1=mask_ap, op=ALU.mult), reads=[AB_, CONST], writes=[AB_])
            av_fn(ab, AB_)
            if not last:
                P.op("pe", lambda e: e.matmul(pacc[0:128, 0:width], lstr_b[0:jl, 0:128], spb[0:jl, 0:width], start=False, stop=True), reads=[SPB, CONST], writes=[PACC])

        for h in range(H):
            kth, KTH = kth_pool.next()
            kc_ = (h * 128) // KROWS; r_ = h * 128 - kc_ * KROWS
            P.dma("sp", kth[:, 0:TC], bk_out[kc_].ap()[r_:r_ + 128, :], reads=[BKO[kc_]], writes=[KTH])
            P.dma("sp", kth[:, TC:TC + TO], bk_in[kc_].ap()[r_:r_ + 128, :], reads=[BKI[kc_]], writes=[KTH])
            vh, VH = vh_pool.next()
            vhv = vh[:, :].rearrange("p (b d) -> p b d", d=128)
            nbk = VROWS // 128
            for i in range(nkc):
                P.dma("sp", vhv[:, i * nbk:(i + 1) * nbk, :], bv_out[i].ap()[0:VROWS, h * 128:(h + 1) * 128].rearrange("(b p) d -> p b d", p=128),
                      reads=[BVO[i]], writes=[VH])
                P.dma("sp", vhv[:, TC // 128 + i * nbk:TC // 128 + (i + 1) * nbk, :], bv_in[i].ap()[:, h * 128:(h + 1) * 128].rearrange("(b p) d -> p b d", p=128),
                      reads=[BVI[i]], writes=[VH])
            P.op("dve", lambda e: e.tensor_scalar(out=vh[:, 0:TC], in0=vh[:, 0:TC], scalar1=vfl[:, 0:1], scalar2=None, op0=ALU.mult), reads=[VH, CONST], writes=[VH])
            for (q0, ql) in ftiles(0, TO):
                kb_hi = (TC + q0 + ql) // 128 - 1
                kb_d0 = (TC + q0) // 128
                for kb in range(kb_hi, -1, -1):
                    o = kb - kb_d0
                    first = (kb == kb_hi); last = (kb == 0)
                    Zp, ZB = pf_pool.next()
                    P.op("pe", lambda e, kb=kb: e.matmul(Zp[:, 0:ql], kth[:, kb * 128:(kb + 1) * 128], qT1[:, h, q0:q0 + ql], start=True, stop=True), reads=[KTH, B2], writes=[ZB])

                    def av(ab, AB_, kb=kb, first=first, last=last):
                        P.op("pe", lambda e: e.matmul(pout[:, 0:ql], vhv[:, kb, :], ab[:, 0:ql], start=first, stop=last), reads=[VH, AB_], writes=[POUT])
                    sb_block(Zp, ZB, ql, 128, first, last, bias_bc[:, h:h + 1], amv[:, o, 0:ql] if o >= 0 else None, av)
                P.op("act", lambda e: e.activation(out=aT[:, h, q0:q0 + ql], in_=pout[:, 0:ql], func=AF.Copy), reads=[POUT], writes=[B1])

        pop_scope()
        ckpt(P, "attn")
        push_scope()
        PT = Buf("pt")
        ptb = sb("ptb", [128, NPAGES], I32); ptf = sb("ptf", [128, NPAGES], F32)
        iop = sb("iop", [128, 1], I32); iof = sb("iof", [128, 1], F32)
        pidx = sb("pidx", [128, NPAGES], I32)
        P.dma("sp", ptb[:, :], ptab[0, :].partition_broadcast(128), writes=[PT])
        P.op("pool", lambda e: e.iota(iop[:, :], [[0, 1]], base=0, channel_multiplier=1), writes=[PT])
        P.op("dve", lambda e: e.tensor_copy(out=ptf[:, :], in_=ptb[:, :]), reads=[PT], writes=[PT])
        P.op("dve", lambda e: e.tensor_copy(out=iof[:, :], in_=iop[:, :]), reads=[PT], writes=[PT])
        P.op("dve", lambda e: e.tensor_scalar(out=pidx[:, :], in0=ptf[:, :], scalar1=128.0, scalar2=iof[:, 0:1], op0=ALU.mult, op1=ALU.add),
             reads=[PT], writes=[PT])
        ck2 = cache_k.rearrange("n p d -> (n p) d"); cv2 = cache_v.rearrange("n p d -> (n p) d")
        kpb_pool = Rot([(sb("kpb%d" % i, [128, D], BF16), Buf("kpb%d" % i)) for i in range(2)])
        vpb_pool = Rot([(sb("vpb%d" % i, [128, D], BF16), Buf("vpb%d" % i)) for i in range(2)])
        ktp_pool = Rot([(sb("ktp%d" % i, [128, D], BF16), Buf("ktp%d" % i)) for i in range(2)])
        W8 = H * 8
        Zp, ZB = pf_pool.next()
        for h in range(H):
            P.op("pe", lambda e, h=h: e.matmul(Zp[0:TS, h * 8:(h + 1) * 8], kTsv[:, h, :], qT1[:, h, TO:TO + TS], start=(h == 0), stop=False), reads=[KTS, B2], writes=[ZB])
        P.op("pe", lambda e: e.matmul(Zp[0:TS, 0:W8], ones_f[0:1, 0:TS], biasrow[0:1, :], start=False, stop=True), reads=[CONST], writes=[ZB])

        def av_new(ab, AB_):
            for h in range(H):
                P.op("pe", lambda e, h=h: e.matmul(pout[:, h * 8:(h + 1) * 8], Vs[0:TS, h * 128:(h + 1) * 128], ab[0:TS, h * 8:(h + 1) * 8], start=(h == 0), stop=False),
                     reads=[VS, AB_], writes=[POUT])
        sb_block(Zp, ZB, W8, TS, True, False, None, ams_b[0:TS, :], av_new)
        for pg in range(NPAGES - 1, -1, -1):
            last = (pg == 0)
            kpb, KPB = kpb_pool.next(); vpb, VPB = vpb_pool.next()
            P.idma(kpb[:, :], ck2[:, :], pidx[:, pg:pg + 1], reads=[PT], writes=[KPB])
            P.idma(vpb[:, :], cv2[:, :], pidx[:, pg:pg + 1], reads=[PT], writes=[VPB])
            ktp, KTP = ktp_pool.next()
            transpose_to(lambda c: ktp[:, c * 128:(c + 1) * 128], KTP, kpb, KPB, 128, H)
            Zp, ZB = pf_pool.next()
            for h in range(H):
                P.op("pe", lambda e, h=h: e.matmul(Zp[:, h * 8:(h + 1) * 8], ktp[:, h * 128:(h + 1) * 128], qT1[:, h, TO:TO + TS], start=(h == 0), stop=False), reads=[KTP, B2], writes=[ZB])
            P.op("pe", lambda e: e.matmul(Zp[:, 0:W8], ones_f[0:1, 0:128], biasrow[0:1, :], start=False, stop=True), reads=[CONST], writes=[ZB])

            def av_pg(ab, AB_, last=last, vpb=vpb, VPB=VPB):
                for h in range(H):
                    P.op("pe", lambda e, h=h: e.matmul(pout[:, h * 8:(h + 1) * 8], vpb[:, h * 128:(h + 1) * 128], ab[:, h * 8:(h + 1) * 8], start=False, stop=last),
                         reads=[VPB, AB_], writes=[POUT])
            sb_block(Zp, ZB, W8, 128, False, last, None, None, av_pg)
        P.op("act", lambda e: e.activation(out=aT[:, :, TO:TO + TS], in_=pout[:, 0:W8].rearrange("p (h t) -> p h t", t=8), func=AF.Copy), reads=[POUT], writes=[B1])

        pop_scope()
        pop_scope()
        ckpt(P, "sattn")
        out_proj(w_out_b, KC, aT, B1)
        ckpt(P, "l1op")
        post(1, 0, x_d, XD)
        ffn(1, x_d, XD, final_out=y_own)
        P.skip = False
        if os.environ.get("KDBG") == "xd":
            for (t0, l) in ttiles(T):
                xs, XB = xs_pool.next()
                P.dma("sp", xs[0:l, :], x_d[t0:t0 + l, :], reads=[XD], writes=[XB], group="xs")
                P.dma("sp", y_own[t0:t0 + l, :], xs[0:l, :], reads=[XB], writes=[OUTB], group="out")
        if os.environ.get("KDBG") == "gb":
            P.dma("sp", y_own[TO:TO + TS, :], gb_t[0:TS, :], reads=[GB], writes=[OUTB], group="out")
        if os.environ.get("KDBG") == "att":
            P.dma("pool", y_own[0:128, 0:H * TS].rearrange("p (h t) -> p h t", t=TS), aT[:, :, TO:TO + TS], reads=[B1], writes=[OUTB])
        if os.environ.get("KDBG") == "bko":
            xs, XB = xs_pool.next()
            P.dma("pool", xs[:, 0:TO], bk_out[0].ap()[0:128, :], reads=[BKO[0]], writes=[XB])
            P.dma("sp", y_own[0:128, 0:TO], xs[:, 0:TO], reads=[XB], writes=[OUTB])
            xs, XB = xs_pool.next()
            P.dma("pool", xs[:, 0:TO], bk_out[0].ap()[KROWS:KROWS + 128, :], reads=[BKO[0]], writes=[XB])
            P.dma("sp", y_own[128:256, 0:TO], xs[:, 0:TO], reads=[XB], writes=[OUTB])
            xs, XB = xs_pool.next()
            P.dma("pool", xs[:, 0:TO], bk_in[0].ap()[0:128, :], reads=[BKI[0]], writes=[XB])
            P.dma("sp", y_own[256:384, 0:TO], xs[:, 0:TO], reads=[XB], writes=[OUTB])
        if os.environ.get("KDBG") == "od":
            for (t0, l) in ttiles(T):
                xs, XB = xs_pool.next()
                P.dma("sp", xs[0:l, :], o_d[t0:t0 + l, :], reads=[OD], writes=[XB], group="xs")
                P.dma("sp", y_own[t0:t0 + l, :], xs[0:l, :], reads=[XB], writes=[OUTB], group="out")
        if os.environ.get("KSTATS"):
            print("ENGINE COUNTS", {k: v.count for k, v in P.engs.items()}, "dma sems", len(P.dsems), "max dma count", max(d.count for d in P.dsems), "free sems", len(P.free_sems))
        P.wait_all_dma("sp")
        for en in ("pe", "act", "dve", "pool"):
            for eo in P.engs[en].epochs:
                if eo.count > 0:
                    nc.sync.wait_ge(eo.sem, eo.count)
    return nc


def make_consts(H):
    j = np.arange(128)[:, None]; t = np.arange(128)[None, :]
    ident = np.eye(128, dtype=np.float32)
    m1 = (j <= t).astype(np.float32)
    uincl = (j >= t).astype(np.float32)
    lstr = np.ones((128, 128), np.float32)
    lstr = (j < t).astype(np.float32)
    tt = np.arange(512)[None, None, :]
    oo = np.arange(4)[None, :, None]
    jj = np.arange(128)[:, None, None]
    am = ((oo * 128 + jj) < tt).astype(np.float32).reshape(128, 4 * 512)
    j8 = np.arange(8)[:, None, None]; t8 = np.arange(8)[None, None, :]
    ams = np.broadcast_to((j8 < t8), (8, H, 8)).astype(np.float32).reshape(8, H * 8)
    return dict(c_ident=ident, c_m1=m1, c_uincl=uincl, c_lstr=lstr, c_am=np.ascontiguousarray(am), c_ams=np.ascontiguousarray(ams))


_PROG_CACHE = {}


def run(cfg, x_prompt, x_sample, state_C, state_n, state_m, cache_k, cache_v, page_table, c_prompt, c_sample, w_ada, b_ada,
        g_norm, w_in_a, b_gate_a, g_head_a, w_out_a, w_in_b, b_sb, w_out_b, w_ffn_in, w_ffn_out):
    D = cfg["D"]; SEQ = cfg["SEQ"]; TS = cfg["TS"]; MH = cfg["MH"]
    TO = SEQ // 2
    H = D // 128
    NB = x_prompt.shape[0]
    NPOOL = cfg["NPOOL"]
    key = tuple(sorted(cfg.items()))
    if key not in _PROG_CACHE:
        _PROG_CACHE[key] = build_program(cfg)
    nc = _PROG_CACHE[key]
    f = np.float32
    consts = make_consts(H)
    ck = np.ascontiguousarray(np.asarray(cache_k[0], f).reshape(NPOOL, 128, D))
    cv = np.ascontiguousarray(np.asarray(cache_v[0], f).reshape(NPOOL, 128, D))
    shared = dict(
        cache_k=ck, cache_v=cv, w_ada=np.asarray(w_ada, f), b_ada=np.asarray(b_ada, f), g_norm=np.asarray(g_norm, f),
        w_in_a=np.asarray(w_in_a[0], f), b_gate=np.ascontiguousarray(np.asarray(b_gate_a[0], f).T), g_head=np.asarray(g_head_a[0], f),
        w_out_a=np.asarray(w_out_a[0], f), w_in_b=np.asarray(w_in_b[0], f), b_sb=np.asarray(b_sb[0], f), w_out_b=np.asarray(w_out_b[0], f),
        w_ffn_in=np.asarray(w_ffn_in, f), w_ffn_out=np.asarray(w_ffn_out, f), **consts)
    in_maps = []
    for c in range(8):
        b = c // 2; p = c % 2
        m = dict(shared)
        m["x_own"] = np.ascontiguousarray(np.concatenate([x_prompt[b, p * TO:(p + 1) * TO], x_sample[c]], axis=0).astype(f))
        m["x_pre"] = np.ascontiguousarray(np.asarray(x_prompt[b, 0:TO], f))
        m["c2"] = np.ascontiguousarray(np.stack([c_prompt[b], c_sample[c]]).astype(f))
        fl = np.zeros((MH, 2), f)
        fl[:, 0] = float(p); fl[:, 1] = 0.0 if p == 1 else -1e30
        m["flags"] = fl
        m["vflag"] = np.full((128, 1), float(p), f)
        m["st_C"] = np.ascontiguousarray(np.asarray(state_C[0, c], f))
        m["st_n"] = np.ascontiguousarray(np.asarray(state_n[0, c], f))
        m["st_m"] = np.ascontiguousarray(np.asarray(state_m[0, c], f).reshape(MH, 1))
        m["ptab"] = np.ascontiguousarray(np.asarray(page_table[c], np.int32).reshape(1, -1))
        in_maps.append(m)
    res = run_bass_kernel_spmd(nc, in_maps, core_ids=list(range(8)))
    R = res.results
    y_prompt = np.zeros((NB, SEQ, D), f); k_prompt = np.zeros((1, NB, SEQ, H, 128), f); v_prompt = np.zeros((1, NB, SEQ, H, 128), f)
    y_sample = np.zeros((8, TS, D), f); k_sample = np.zeros((1, 8, TS, H, 128), f); v_sample = np.zeros((1, 8, TS, H, 128), f)
    C_prompt = np.zeros((1, NB, MH, 128, 256), f); n_prompt = np.zeros((1, NB, MH, 128), f); m_prompt = np.zeros((1, NB, MH), f)
    C_sample = np.zeros((1, 8, MH, 128, 256), f); n_sample = np.zeros((1, 8, MH, 128), f); m_sample = np.zeros((1, 8, MH), f)
    for c in range(8):
        b = c // 2; p = c % 2
        r = R[c]
        y_prompt[b, p * TO:(p + 1) * TO] = r["y_own"][0:TO]
        y_sample[c] = r["y_own"][TO:TO + TS]
        k_prompt[0, b, p * TO:(p + 1) * TO] = r["k_own"].reshape(TO, H, 128)
        v_prompt[0, b, p * TO:(p + 1) * TO] = r["v_own"].reshape(TO, H, 128)
        k_sample[0, c] = r["k_s"].reshape(TS, H, 128)
        v_sample[0, c] = r["v_s"].reshape(TS, H, 128)
        if p == 1:
            C_prompt[0, b] = r["C_p"]; n_prompt[0, b] = r["n_p"]; m_prompt[0, b] = r["m_p"][:, 0]
        C_sample[0, c] = r["C_s"]; n_sample[0, c] = r["n_s"]; m_sample[0, c] = r["m_s"][:, 0]
    return (y_prompt, y_sample, k_prompt, v_prompt, k_sample, v_sample, C_prompt, n_prompt, m_prompt, C_sample, n_sample, m_sample)


def kernel(**inputs):
    return run(FULL_CFG, **{k: np.asarray(v) for k, v in inputs.items()})
```
